# Optimizing a Trainium2 kernel written in Bass

```python
import math
import jax, jax.numpy as jnp
from jax import lax
import numpy as np

D_MODEL = 1024
BATCH = 4
SEQ = 8192
DEPTH = 4

GRID_W = 64
CTX_LEN = 256
HEAD_DIM = 64
HALF = HEAD_DIM // 2
AXIS_DIM = HEAD_DIM // 2
ROPE_THETA = 10000.0
DA_HEADS = 4
GQ_HEADS = 8
GKV_HEADS = 2
HY_WIDTH = 512
HY_ORDER = 2
HY_EMB = 33
HY_BANDS = (HY_EMB - 1) // 2
HY_HIDDEN = 64
HY_MIN_DECAY = math.log(1e-2) / 1.5
HY_MAX_DECAY = math.log(1e-2) / 0.3
N_BRANCH = 3
D_FF = 2816
Q_BLOCK = 128
LN_EPS = 1e-6
DEEPNORM_ALPHA = (2 * DEPTH) ** 0.25
DEEPNORM_BETA = (8 * DEPTH) ** -0.25

DA_QK = DA_HEADS * 2 * HEAD_DIM
DA_V = DA_HEADS * 2 * HEAD_DIM
GQ_Q = GQ_HEADS * HEAD_DIM
GQ_KV = GKV_HEADS * HEAD_DIM
BRANCH_W = 512
SPLITS = (DA_QK, DA_V, GQ_KV, GQ_KV, DA_QK, GQ_Q, 3 * HY_WIDTH, N_BRANCH * D_MODEL)
IN_COLS = sum(SPLITS)
KV_COLS = sum(SPLITS[:4])

kernel_name = 'hybrid_diffattn_gqa_hyena_convffn_dit'

F32 = jnp.float32


def layer_norm(x, g=None, b=None):
    x32 = x.astype(F32)
    xc = x32 - jnp.mean(x32, axis=-1, keepdims=True)
    y = xc * lax.rsqrt(jnp.mean(xc * xc, axis=-1, keepdims=True) + LN_EPS)
    if g is not None:
        y = y * g + b
    return y.astype(x.dtype)


def rms_norm(x, g):
    x32 = x.astype(F32)
    y = x32 * lax.rsqrt(jnp.mean(x32 * x32, axis=-1, keepdims=True) + LN_EPS)
    return (y * g).astype(x.dtype)


def modulate(x, shift, scale):
    return layer_norm(x) * (1.0 + scale) + shift


def split_cols(p, sizes):
    offs = [int(v) for v in np.cumsum(sizes)[:-1]]
    return jnp.split(p, offs, axis=-1)


def flat_heads(t):
    return t.reshape(t.shape[0], t.shape[1], -1)


def axial_rope(n):
    rows = n // GRID_W
    r = jnp.repeat(jnp.arange(rows, dtype=F32), GRID_W)
    col = jnp.tile(jnp.arange(GRID_W, dtype=F32), rows)
    inv = ROPE_THETA ** (-jnp.arange(0, AXIS_DIM, 2, dtype=F32) / AXIS_DIM)
    ang = jnp.concatenate([r[:, None] * inv, col[:, None] * inv], axis=-1)
    return jnp.cos(ang), jnp.sin(ang)


def apply_rope(x, rope):
    cos, sin = rope
    x32 = x.astype(F32)
    x1, x2 = x32[..., :HALF], x32[..., HALF:]
    c_, s_ = cos[:, None, :], sin[:, None, :]
    return jnp.concatenate([x1 * c_ - x2 * s_, x1 * s_ + x2 * c_], axis=-1).astype(x.dtype)


def dw_conv3(x, w, b):
    xp = jnp.pad(x, ((0, 0), (1, 1), (0, 0)))
    return xp[:, :-2] * w[0] + xp[:, 1:-1] * w[1] + xp[:, 2:] * w[2] + b


def split_pair(t):
    b, n = t.shape[:2]
    t = t.reshape(b, n, DA_HEADS, 2, HEAD_DIM)
    return t[..., 0, :], t[..., 1, :]


def prep_kv(ka, va, kb, vb, gq_kn, rope):
    b, n = ka.shape[:2]
    k1, k2 = split_pair(ka)
    va = va.reshape(b, n, DA_HEADS, 2 * HEAD_DIM)
    kb = rms_norm(kb.reshape(b, n, GKV_HEADS, HEAD_DIM), gq_kn)
    vb = vb.reshape(b, n, GKV_HEADS, HEAD_DIM)
    if rope is not None:
        k1, k2, kb = apply_rope(k1, rope), apply_rope(k2, rope), apply_rope(kb, rope)
    return [k1, k2, va, kb, vb]


def prep_q(qa, qb, gq_qn, rope):
    b, n = qa.shape[:2]
    q1, q2 = split_pair(qa)
    qb = rms_norm(qb.reshape(b, n, GQ_HEADS, HEAD_DIM), gq_qn)
    if rope is not None:
        q1, q2, qb = apply_rope(q1, rope), apply_rope(q2, rope), apply_rope(qb, rope)
    return q1, q2, qb.reshape(b, n, GKV_HEADS, GQ_HEADS // GKV_HEADS, HEAD_DIM)


def diff_core(q1, q2, k1, k2, v, lam):
    sc = HEAD_DIM ** -0.5
    p1 = jax.nn.softmax(jnp.einsum('bqhd,bkhd->bhqk', q1, k1).astype(F32) * sc, axis=-1)
    p2 = jax.nn.softmax(jnp.einsum('bqhd,bkhd->bhqk', q2, k2).astype(F32) * sc, axis=-1)
    p = (p1 - lam * p2).astype(v.dtype)
    return jnp.einsum('bhqk,bkhd->bqhd', p, v)


def gqa_core(q, k, v):
    s = jnp.einsum('bqhgd,bkhd->bhgqk', q, k).astype(F32) * (HEAD_DIM ** -0.5)
    p = jax.nn.softmax(s, axis=-1).astype(v.dtype)
    return jnp.einsum('bhgqk,bkhd->bqhgd', p, v)


def sweep_queries(core, qs, kvs):
    b, n = qs[0].shape[:2]
    nb = n // Q_BLOCK
    blocks = tuple(jnp.swapaxes(q.reshape(b, nb, Q_BLOCK, *q.shape[2:]), 0, 1) for q in qs)
    out = lax.map(lambda qb: core(*qb, *kvs), blocks)
    return jnp.swapaxes(out, 0, 1).reshape(b, n, *out.shape[3:])


def diff_post(o, g, lam_init):
    return flat_heads(rms_norm(o, g) * (1.0 - lam_init))


def hyena_filters(n, w1, b1, w2, b2, w3, freq):
    t = jnp.linspace(0.0, 1.0, n, dtype=F32)[:, None]
    f = jnp.linspace(1e-4, HY_BANDS - 1, HY_BANDS, dtype=F32)
    ang = (2.0 * math.pi / n) * jnp.arange(n, dtype=F32)[:, None] * f[None, :]
    z = jnp.concatenate([t, jnp.cos(ang), -jnp.sin(ang)], axis=-1)
    hid = jnp.sin(freq * (z @ w1 + b1))
    hid = jnp.sin(freq * (hid @ w2 + b2))
    h = (hid @ w3).astype(F32).reshape(n, HY_ORDER, 2, HY_WIDTH)
    deltas = jnp.abs(jnp.linspace(HY_MIN_DECAY, HY_MAX_DECAY, HY_WIDTH, dtype=F32))
    h = h * jnp.exp(-t * deltas)[:, None, None, :]
    h = h / jnp.sum(jnp.abs(h), axis=(0, 2), keepdims=True)
    return jnp.moveaxis(h, 0, 2)


def long_conv(z, h_fwd, h_bwd, bias):
    n = z.shape[1]
    filt = jnp.concatenate([h_fwd, jnp.zeros_like(h_fwd[:1]), h_bwd[:0:-1]], axis=0)
    zf = jnp.fft.rfft(z.astype(F32), n=2 * n, axis=1)
    hf = jnp.fft.rfft(filt, n=2 * n, axis=0)
    y = jnp.fft.irfft(zf * hf[None], n=2 * n, axis=1)[:, :n]
    return (y + z.astype(F32) * bias).astype(z.dtype)


def hyena_mix(u, filt, conv_w, conv_b, bias):
    v, x1, x2 = jnp.split(dw_conv3(u, conv_w, conv_b), 3, axis=-1)
    z = x1 * long_conv(v, filt[0, 0], filt[0, 1], bias[0])
    return x2 * long_conv(z, filt[1, 0], filt[1, 1], bias[1])


def branch_merge(a, b_, c_, gates, w_pa, w_pb, w_pc, w_o):
    ga, gb, gc = jnp.split(jax.nn.sigmoid(gates), N_BRANCH, axis=-1)
    m = ga * (a @ w_pa) + gb * (b_ @ w_pb) + gc * (c_ @ w_pc)
    return m @ w_o


def conv_ffn(h, w_up, conv_w, conv_b, w_down):
    val, gate = jnp.split(dw_conv3(h @ w_up, conv_w, conv_b), 2, axis=-1)
    return (jax.nn.silu(gate) * val) @ w_down


def setup_inputs(seed: int = 0) -> dict:
    key = jax.random.key(seed)
    ks = iter(jax.random.split(key, 40))
    d, L = D_MODEL, DEPTH

    def nrm(shape, s):
        return jax.random.normal(next(ks), shape, F32) * s

    def gain(shape):
        return 1.0 + nrm(shape, 0.02)

    return {
        'x': nrm((BATCH, SEQ, d), 1.0),
        'c': nrm((BATCH, d), 1.0),
        'ctx': nrm((BATCH, CTX_LEN, d), 1.0),
        'c_ctx': nrm((d,), 1.0),
        'w_mod': nrm((L, d, 6 * d), d ** -0.5),
        'b_mod': nrm((L, 6 * d), 0.02),
        'w_in': nrm((L, d, IN_COLS), d ** -0.5),
        'da_lq1': nrm((L, HEAD_DIM), 0.1),
        'da_lk1': nrm((L, HEAD_DIM), 0.1),
        'da_lq2': nrm((L, HEAD_DIM), 0.1),
        'da_lk2': nrm((L, HEAD_DIM), 0.1),
        'da_subln': gain((L, 2 * HEAD_DIM)),
        'gq_qn': gain((L, HEAD_DIM)),
        'gq_kn': gain((L, HEAD_DIM)),
        'hy_conv_w': nrm((L, 3, 3 * HY_WIDTH), 3 ** -0.5),
        'hy_conv_b': nrm((L, 3 * HY_WIDTH), 0.02),
        'hy_w1': nrm((L, HY_EMB, HY_HIDDEN), HY_EMB ** -0.5),
        'hy_b1': nrm((L, HY_HIDDEN), 0.02),
        'hy_w2': nrm((L, HY_HIDDEN, HY_HIDDEN), HY_HIDDEN ** -0.5),
        'hy_b2': nrm((L, HY_HIDDEN), 0.02),
        'hy_w3': nrm((L, HY_HIDDEN, HY_ORDER * 2 * HY_WIDTH), HY_HIDDEN ** -0.5),
        'hy_freq': gain((L, HY_HIDDEN)),
        'hy_bias': nrm((L, HY_ORDER, HY_WIDTH), 0.1),
        'w_pa': nrm((L, BRANCH_W, d), BRANCH_W ** -0.5),
        'w_pb': nrm((L, BRANCH_W, d), BRANCH_W ** -0.5),
        'w_pc': nrm((L, BRANCH_W, d), BRANCH_W ** -0.5),
        'w_o': nrm((L, d, d), DEEPNORM_BETA * d ** -0.5),
        'ln1_g': gain((L, d)),
        'ln1_b': nrm((L, d), 0.02),
        'w_up': nrm((L, d, 2 * D_FF), d ** -0.5),
        'ffn_conv_w': nrm((L, 3, 2 * D_FF), 3 ** -0.5),
        'ffn_conv_b': nrm((L, 2 * D_FF), 0.02),
        'w_down': nrm((L, D_FF, d), DEEPNORM_BETA * D_FF ** -0.5),
        'ln2_g': gain((L, d)),
        'ln2_b': nrm((L, d), 0.02),
    }


def reference(x, c, ctx, c_ctx, w_mod, b_mod, w_in, da_lq1, da_lk1, da_lq2, da_lk2, da_subln,
              gq_qn, gq_kn, hy_conv_w, hy_conv_b, hy_w1, hy_b1, hy_w2, hy_b2, hy_w3, hy_freq, hy_bias,
              w_pa, w_pb, w_pc, w_o, ln1_g, ln1_b, w_up, ffn_conv_w, ffn_conv_b, w_down, ln2_g, ln2_b):
    n_lat = x.shape[1]
    n_ctx = ctx.shape[1]
    rope = axial_rope(n_lat)
    xc = ctx
    for l in range(DEPTH):
        last = l == DEPTH - 1
        lam_init = 0.8 - 0.6 * math.exp(-0.3 * l)
        lam = (jnp.exp(jnp.sum(da_lq1[l].astype(F32) * da_lk1[l].astype(F32)))
               - jnp.exp(jnp.sum(da_lq2[l].astype(F32) * da_lk2[l].astype(F32))) + lam_init)
        hy = (hy_w1[l], hy_b1[l], hy_w2[l], hy_b2[l], hy_w3[l], hy_freq[l])

        mod = (jax.nn.silu(c) @ w_mod[l] + b_mod[l])[:, None, :]
        sh1, sc1, g1, sh2, sc2, g2 = jnp.split(mod, 6, axis=-1)
        n_mc = 2 if last else 6
        mc = jnp.split(jax.nn.silu(c_ctx) @ w_mod[l, :, :n_mc * D_MODEL] + b_mod[l, :n_mc * D_MODEL], n_mc)

        hc = modulate(xc, mc[0], mc[1])
        pc = split_cols(hc @ w_in[l, :, :(KV_COLS if last else IN_COLS)], SPLITS[:4] if last else SPLITS)
        kv_c = prep_kv(pc[0], pc[1], pc[2], pc[3], gq_kn[l], None)

        hl = modulate(x, sh1, sc1)
        pl = split_cols(hl @ w_in[l], SPLITS)
        kv_l = prep_kv(pl[0], pl[1], pl[2], pl[3], gq_kn[l], rope)
        k1, k2, va, kb, vb = [jnp.concatenate([a, b], axis=1) for a, b in zip(kv_c, kv_l)]
        q1, q2, qb = prep_q(pl[4], pl[5], gq_qn[l], rope)
        a_l = diff_post(sweep_queries(diff_core, (q1, q2), (k1, k2, va, lam)), da_subln[l], lam_init)
        b_l = flat_heads(sweep_queries(gqa_core, (qb,), (kb, vb)))
        c_l = hyena_mix(pl[6], hyena_filters(n_lat, *hy), hy_conv_w[l], hy_conv_b[l], hy_bias[l])
        out_l = branch_merge(a_l, b_l, c_l, pl[7], w_pa[l], w_pb[l], w_pc[l], w_o[l])
        x = layer_norm(DEEPNORM_ALPHA * x + g1 * out_l, ln1_g[l], ln1_b[l])
        f_l = conv_ffn(modulate(x, sh2, sc2), w_up[l], ffn_conv_w[l], ffn_conv_b[l], w_down[l])
        x = layer_norm(DEEPNORM_ALPHA * x + g2 * f_l, ln2_g[l], ln2_b[l])

        if not last:
            q1c, q2c, qbc = prep_q(pc[4], pc[5], gq_qn[l], None)
            a_c = diff_post(diff_core(q1c, q2c, kv_c[0], kv_c[1], kv_c[2], lam), da_subln[l], lam_init)
            b_c = flat_heads(gqa_core(qbc, kv_c[3], kv_c[4]))
            c_c = hyena_mix(pc[6], hyena_filters(n_ctx, *hy), hy_conv_w[l], hy_conv_b[l], hy_bias[l])
            out_c = branch_merge(a_c, b_c, c_c, pc[7], w_pa[l], w_pb[l], w_pc[l], w_o[l])
            xc = layer_norm(DEEPNORM_ALPHA * xc + mc[2] * out_c, ln1_g[l], ln1_b[l])
            f_c = conv_ffn(modulate(xc, mc[3], mc[4]), w_up[l], ffn_conv_w[l], ffn_conv_b[l], w_down[l])
            xc = layer_norm(DEEPNORM_ALPHA * xc + mc[5] * f_c, ln2_g[l], ln2_b[l])
    return x
```

```python
import math
import os
from contextlib import ExitStack
import numpy as np
import ml_dtypes
import concourse.bass as bass
import concourse.mybir as mybir
from concourse.bass_utils import run_bass_kernel_spmd

F32, BF16 = mybir.dt.float32, mybir.dt.bfloat16
AF = mybir.ActivationFunctionType
ALU = mybir.AluOpType
AX = mybir.AxisListType
NPBF = ml_dtypes.bfloat16

D = 1024; NL = 8192; NC_ = 256; U = NL + NC_; DEPTH = 4
DFF = 2816; HY = 512
EPS = 1e-6
ALPHA = (2 * DEPTH) ** 0.25
SEM_LIMIT = 24000
SERIAL_MODE = True


class Res:
    __slots__ = ("name", "W", "R", "dkey")

    def __init__(self, name):
        self.name = name; self.W = {}; self.R = {}; self.dkey = None


class Sched:
    def __init__(self, nc, stack):
        self.nc = nc; self.stack = stack
        self.eng = {"pe": nc.tensor, "act": nc.scalar, "dve": nc.vector, "pool": nc.gpsimd, "sp": nc.sync}
        self.esem = {k: stack.enter_context(nc.semaphore("e_" + k)) for k in self.eng}
        self.cnt = {k: 0 for k in self.eng}
        self.seen = {k: {} for k in self.eng}
        self.dsems = {}
        self.dfree = []
        self.dn = 0
        self.bar = stack.enter_context(nc.semaphore("bar")); self.barcnt = 0
        self.res = []
        self.nins = 0
        self.uid = 0

    def R(self, name):
        r = Res(name); self.res.append(r); return r

    def _wait(self, e, deps):
        for key, val in deps.items():
            if key[0] == "d":
                h, val = self.dsems[key]
            else:
                if key[1] == e and e == "pe":
                    continue
                h = self.esem[key[1]]
            if self.seen[e].get(key, 0) >= val:
                continue
            self.eng[e].wait_ge(h, val); self.seen[e][key] = val; self.nins += 1

    @staticmethod
    def _merge(d, s):
        for k, v in s.items():
            if d.get(k, 0) < v:
                d[k] = v

    def op(self, e, fn, reads=(), writes=()):
        deps = {}
        for r in reads:
            self._merge(deps, r.W)
        for w in writes:
            self._merge(deps, w.W); self._merge(deps, w.R)
        if SERIAL_MODE:
            for e2 in self.eng:
                if self.cnt[e2] > 0:
                    deps[("e", e2)] = self.cnt[e2]
            for k2, (h2, t2) in self.dsems.items():
                if t2 > 0:
                    deps[k2] = t2
        self._wait(e, deps)
        ins = fn(self.eng[e])
        self.cnt[e] += 1; self.nins += 1
        ins.then_inc(self.esem[e], 1)
        key = ("e", e); v = self.cnt[e]
        for r in reads:
            r.R[key] = v
        for w in writes:
            w.W[key] = v
        if self.cnt[e] >= SEM_LIMIT:
            self.reset()

    def dma(self, q, out, in_, sb, reads=(), writes=(), **kw):
        deps = {}
        for r in reads:
            self._merge(deps, r.W)
        for w in writes:
            self._merge(deps, w.W); self._merge(deps, w.R)
        self._wait(q, deps)
        if sb.dkey is None:
            sb.dkey = ("d", self.dn); self.dn += 1
            if self.dfree:
                h = self.dfree.pop()
            else:
                h = self.stack.enter_context(self.nc.semaphore("d%d" % self.dn))
            self.dsems[sb.dkey] = [h, 0]
        ent = self.dsems[sb.dkey]
        ent[1] += 16
        self.eng[q].dma_start(out=out, in_=in_, **kw).then_inc(ent[0], 16)
        self.nins += 1
        for r in reads:
            r.R[sb.dkey] = ent[1]
        for w in writes:
            w.W[sb.dkey] = ent[1]
        if ent[1] >= SEM_LIMIT:
            self.reset()

    def reset(self, final=False):
        for e, en in self.eng.items():
            if self.cnt[e] > 0:
                en.wait_ge(self.esem[e], self.cnt[e])
        for key, (h, tot) in self.dsems.items():
            if tot > 0:
                self.eng["sp"].wait_ge(h, tot)
        self.barcnt += 1
        for e, en in self.eng.items():
            en.sem_inc(self.bar, 1)
        for e, en in self.eng.items():
            en.wait_ge(self.bar, 5 * self.barcnt)
        if final:
            return
        sp = self.eng["sp"]
        for e in self.eng:
            sp.sem_clear(self.esem[e])
        for key, (h, tot) in self.dsems.items():
            if tot > 0:
                sp.sem_clear(h)
        self.barcnt += 1
        for e, en in self.eng.items():
            en.sem_inc(self.bar, 1)
        for e, en in self.eng.items():
            en.wait_ge(self.bar, 5 * self.barcnt)
        for e in self.eng:
            self.cnt[e] = 0; self.seen[e] = {}
        for key in self.dsems:
            self.dsems[key][1] = 0
        for r in self.res:
            r.W = {}; r.R = {}


    def end_phase(self):
        self.reset()
        for key, (h, tot) in self.dsems.items():
            self.dfree.append(h)
        self.dsems = {}
        for r in self.res:
            r.dkey = None
        self.res = [r for r in self.res if r.name.startswith("DR_") or r.name.startswith("P_")]


class TPool:
    def __init__(self, S, name, shape, dtype, n, psum=False):
        self.t = []; self.r = []; self.i = 0; self.n = n
        S.uid += 1
        for i in range(n):
            nm = "%s_%d_%d" % (name, S.uid, i)
            cm = S.nc.psum_tensor(nm, shape, dtype) if psum else S.nc.sbuf_tensor(nm, shape, dtype)
            self.t.append(S.pst.enter_context(cm)); self.r.append(S.R(nm))

    def get(self):
        i = self.i % self.n; self.i += 1
        return self.t[i], self.r[i]


def rope_tables():
    rows = NL // 64
    r = np.repeat(np.arange(rows, dtype=np.float32), 64)
    col = np.tile(np.arange(64, dtype=np.float32), rows)
    inv = (10000.0 ** (-np.arange(0, 32, 2, dtype=np.float32) / 32)).astype(np.float32)
    ang = np.concatenate([r[:, None] * inv, col[:, None] * inv], axis=-1).astype(np.float32)
    cos, sin = np.cos(ang).astype(np.float32), np.sin(ang).astype(np.float32)
    C = np.ones((128, U), np.float32); Sg = np.zeros((128, U), np.float32)
    for p in range(128):
        i = p % 32
        C[p, NC_:] = cos[:, i]
        Sg[p, NC_:] = sin[:, i] * (-1.0 if (p % 64) < 32 else 1.0)
    return C, Sg


def hyena_pos(n):
    t = np.linspace(0.0, 1.0, n, dtype=np.float32)[:, None]
    f = np.linspace(1e-4, 15, 16, dtype=np.float32)
    ang = (np.float32(2.0 * math.pi / n) * np.arange(n, dtype=np.float32)[:, None] * f[None, :]).astype(np.float32)
    z = np.concatenate([t, np.cos(ang), -np.sin(ang)], axis=-1).astype(np.float32)
    dmin = math.log(1e-2) / 1.5; dmax = math.log(1e-2) / 0.3
    deltas = np.abs(np.linspace(dmin, dmax, HY, dtype=np.float32))
    dec = np.exp(-t * deltas[None, :]).astype(np.float32)
    return z, dec


W_IN_COLS = 6912
QK_SRC = ([("k", h * 128) for h in range(4)] + [("q", 1280 + h * 128) for h in range(4)]
          + [("kb", 1024), ("kb", 1088)] + [("qb", 1792 + c * 128) for c in range(4)])


def build(nlayers=DEPTH, dbg=None, nlw=DEPTH):
    nc = bass.Bass("TRN2", target_bir_lowering=False)
    gst = ExitStack()
    S = Sched(nc, gst)
    E = S.eng

    def dram(name, shape, dt, kind="Internal"):
        return nc.dram_tensor(name, list(shape), dt, kind=kind).ap()

    def din(name, shape, dt=F32):
        return dram(name, shape, dt, "ExternalInput")

    x_in = din("x", [NL, D]); ctx_in = din("ctx", [NC_, D]); cvec = din("cvec", [2, D])
    w_mod = din("w_mod", [nlw, D, 6 * D]); b_mod = din("b_mod", [nlw, 6 * D])
    w_in = din("w_in", [nlw, D, W_IN_COLS])
    lam4 = din("lam4", [nlw, 4, 64])
    da_subln = din("da_subln", [nlw, 128]); gq_qn = din("gq_qn", [nlw, 64]); gq_kn = din("gq_kn", [nlw, 64])
    hy_conv_w = din("hy_conv_w", [nlw, 3, 1536]); hy_conv_b = din("hy_conv_b", [nlw, 1536])
    hy_w1 = din("hy_w1", [nlw, 33, 64]); hy_b1 = din("hy_b1", [nlw, 64]); hy_w2 = din("hy_w2", [nlw, 64, 64])
    hy_b2 = din("hy_b2", [nlw, 64]); hy_w3 = din("hy_w3", [nlw, 64, 2048]); hy_freq = din("hy_freq", [nlw, 64])
    hy_bias = din("hy_bias", [nlw, 2, 512])
    w_pa = din("w_pa", [nlw, 512, D]); w_pb = din("w_pb", [nlw, 512, D]); w_pc = din("w_pc", [nlw, 512, D])
    w_o = din("w_o", [nlw, D, D]); ln1_g = din("ln1_g", [nlw, D]); ln1_b = din("ln1_b", [nlw, D])
    w_up = din("w_up", [nlw, D, 2 * DFF]); ffn_conv_w = din("ffn_conv_w", [nlw, 3, 2 * DFF]); ffn_conv_b = din("ffn_conv_b", [nlw, 2 * DFF])
    w_down = din("w_down", [nlw, DFF, D]); ln2_g = din("ln2_g", [nlw, D]); ln2_b = din("ln2_b", [nlw, D])
    ropeC = din("ropeC", [128, U]); ropeS = din("ropeS", [128, U])
    cbf = din("cbf", [128, 15, 128], BF16)
    cf32 = din("cf32", [128, 6, 128])
    zT = din("zT", [2, 2, 33, NL])
    dec = din("dec", [2, 2, HY, NL])
    y_out = dram("y", [NL, D], F32, "ExternalOutput")

    class _K(dict):
        def __missing__(self, k):
            return "Internal"
    OK_ = _K({k: "ExternalOutput" for k in (dbg or ())})
    xT = [dram("xT0", [D, U], F32, OK_["xT0"]), dram("xT1", [D, U], F32, OK_["xT1"])]
    x1T = dram("x1T", [D, U], F32, OK_["x1T"])
    hT = dram("hT", [D, U], BF16, OK_["hT"])
    qkT = dram("qkT", [14 * 128, U], BF16, OK_["qkT"])
    vtok = dram("vtok", [U, 640], BF16, OK_["vtok"])
    hyT = dram("hyT", [2, 1536, NL], BF16, OK_["hyT"])
    abcT = dram("abcT", [1536, U], BF16, OK_["abcT"])
    w_in_b = dram("w_in_b", [D, W_IN_COLS], BF16); w_pabc_b = dram("w_pabc_b", [1536, D], BF16)
    w_o_b = dram("w_o_b", [D, D], BF16); w_up_b = dram("w_up_b", [D, 2 * DFF], BF16); w_down_b = dram("w_down_b", [DFF, D], BF16)
    gT = dram("gT", [2, 2, HY, 2 * NL], BF16, OK_["gT"])
    Gh = dram("Gh", [2, 2, 128, HY, 2, 128], BF16)

    def sb(name, shape, dt, st=gst):
        S.uid += 1
        return st.enter_context(nc.sbuf_tensor("%s_%d" % (name, S.uid), list(shape), dt))

    CB = sb("CB", [128, 15, 128], BF16); CF = sb("CF", [128, 6, 128], F32)
    modT = sb("modT", [128, 48, 2], F32); mod1p = sb("mod1p", [128, 48, 2], F32)
    vecs = sb("vecs", [128, 160], F32)
    fcw = sb("fcw", [128, 44, 4], F32)
    hcw = sb("hcw", [128, 12, 4], F32)
    R_CB = S.R("P_CB"); R_mod = S.R("P_mod"); R_vecs = S.R("P_vecs")
    ONES, BLK64, ONES128, PERM, F1M0, F1M1, FR, FI, NFI, I2R, I2I = range(11)
    IDENT, TWR, TWI, NTWI, ZERO, NPI = range(6)

    S.dma("sp", CB[:], cbf[:], R_CB, writes=[R_CB])
    S.dma("sp", CF[:], cf32[:], R_CB, writes=[R_CB])

    def act_copy(out, in_):
        return lambda e: e.activation(out=out, in_=in_, func=AF.Copy)

    rot = [0]

    def copy_any(out, in_, reads, writes, engs=("dve", "act")):
        e = engs[rot[0] % len(engs)]; rot[0] += 1
        if e == "act":
            S.op("act", act_copy(out, in_), reads, writes)
        else:
            S.op(e, lambda en: en.tensor_copy(out=out, in_=in_), reads, writes)

    def phase0():
        with ExitStack() as st:
            S.pst = st
            pin = TPool(S, "p0in", [128, 4, D], F32, 2); pps = TPool(S, "p0ps", [128, 512], F32, 4, psum=True)
            pout = TPool(S, "p0out", [128, 8, 512], F32, 2)
            groups = [(ctx_in, 0, 0, 2)] + [(x_in, g * 512, NC_ + g * 512, 4) for g in range(NL // 512)]
            for src, r0, u0, nt in groups:
                xin, rin = pin.get(); n = nt * 128
                S.dma("sp", xin[:, 0:nt, :], src[r0:r0 + n, :].rearrange("(j p) d -> p j d", p=128), rin, writes=[rin])
                xo, ro = pout.get()
                for fc in range(8):
                    ps, rp = pps.get()

                    def tr(pe, ps=ps, xin=xin, fc=fc, nt=nt):
                        for j in range(nt):
                            ins = pe.transpose(ps[:, j * 128:(j + 1) * 128], xin[:, j, fc * 128:(fc + 1) * 128], CF[:, IDENT, :])
                        return ins
                    S.op("pe", tr, reads=[rin, R_CB], writes=[rp])
                    copy_any(xo[:, fc, 0:n], ps[:, 0:n], [rp], [ro])
                S.dma("pool", xT[0][:, u0:u0 + n].rearrange("(fc p) n -> p fc n", p=128), xo[:, :, 0:n], ro, reads=[ro])
            zt = sb("p0z", [128, NL - NC_], BF16, st); rzt = S.R("p0z")
            S.op("pool", lambda e: e.memset(zt[:], 0.0), writes=[rzt])
            for j in range(12):
                S.dma("sp", hyT[1, j * 128:(j + 1) * 128, NC_:NL], zt[:], rzt, reads=[rzt])
        S.end_phase()

    def phase_out(cur):
        with ExitStack() as st:
            S.pst = st
            pin = TPool(S, "pfin", [128, 8, 512], F32, 2); pps = TPool(S, "pfps", [128, 512], F32, 4, psum=True)
            pout = TPool(S, "pfout", [128, 4, D], F32, 2)
            for g in range(NL // 512):
                u0 = NC_ + g * 512
                xi, ri = pin.get()
                S.dma("sp", xi[:], xT[cur][:, u0:u0 + 512].rearrange("(fc p) n -> p fc n", p=128), ri, writes=[ri])
                xo, ro = pout.get()
                for j in range(4):
                    for half in range(2):
                        ps, rp = pps.get()

                        def tr(pe, ps=ps, xi=xi, j=j, half=half):
                            for f in range(4):
                                fc = half * 4 + f
                                ins = pe.transpose(ps[:, f * 128:(f + 1) * 128], xi[:, fc, j * 128:(j + 1) * 128], CF[:, IDENT, :])
                            return ins
                        S.op("pe", tr, reads=[ri, R_CB], writes=[rp])
                        copy_any(xo[:, j, half * 512:(half + 1) * 512], ps[:], [rp], [ro])
                S.dma("pool", y_out[g * 512:(g + 1) * 512, :].rearrange("(j p) d -> p j d", p=128), xo[:], ro, reads=[ro])
        S.end_phase()

    def phase_w(l):
        with ExitStack() as st:
            S.pst = st
            pf = TPool(S, "wf", [128, 2048], F32, 3); pb = TPool(S, "wb", [128, 2048], BF16, 3)
            jobs = [(w_in[l], w_in_b, D, W_IN_COLS), (w_pa[l], w_pabc_b[0:512, :], 512, D), (w_pb[l], w_pabc_b[512:1024, :], 512, D),
                    (w_pc[l], w_pabc_b[1024:1536, :], 512, D), (w_o[l], w_o_b, D, D), (w_up[l], w_up_b, D, 2 * DFF), (w_down[l], w_down_b, DFF, D)]
            k = 0
            for src, dst, Rr, Cc in jobs:
                for r0 in range(0, Rr, 128):
                    for c0 in range(0, Cc, 2048):
                        cw = min(2048, Cc - c0)
                        tf, rf = pf.get(); tb, rb = pb.get()
                        S.dma("sp", tf[:, 0:cw], src[r0:r0 + 128, c0:c0 + cw], rf, writes=[rf])
                        copy_any(tb[:, 0:cw], tf[:, 0:cw], [rf], [rb], engs=("dve", "act"))
                        S.dma("sp", dst[r0:r0 + 128, c0:c0 + cw], tb[:, 0:cw], rb, reads=[rb])
                        k += 1
        S.end_phase()

    NCK = dict(allow_slow_non_contiguous=True)

    def phase_m(l):
        lam_init = 0.8 - 0.6 * math.exp(-0.3 * l)
        with ExitStack() as st:
            S.pst = st
            pw = TPool(S, "mw", [128, 8, 768], F32, 2)
            sc = sb("m_sc", [128, 8, 2], F32, st); r_sc = S.R("m_sc")
            bm = sb("m_b", [128, 48], F32, st); r_bm = S.R("m_b")
            lamt = sb("m_lam", [128, 4, 64], F32, st); r_lam = S.R("m_lam")
            ltmp = sb("m_lt", [128, 2, 64], F32, st); r_lt = S.R("m_lt")
            pps = TPool(S, "mps", [128, 512], F32, 1, psum=True)
            for t in range(2):
                S.dma("sp", sc[:, :, t], cvec[t, :].rearrange("(kc p) -> p kc", p=128), r_sc, writes=[r_sc], **NCK)
            S.dma("sp", bm[:], b_mod[l, :].rearrange("(j p) -> p j", p=128), r_bm, writes=[r_bm], **NCK)
            S.op("act", lambda e: e.activation(out=sc[:], in_=sc[:], func=AF.Silu), reads=[r_sc], writes=[r_sc])
            ps, rp = pps.get()
            for blk in range(8):
                wt, rw = pw.get()
                S.dma("sp", wt[:], w_mod[l, :, blk * 768:(blk + 1) * 768].rearrange("(kc p) n -> p kc n", p=128), rw, writes=[rw])

                def mm(pe, wt=wt, blk=blk):
                    for jj in range(6):
                        j = blk * 6 + jj
                        for kc in range(8):
                            ins = pe.matmul(ps[:, 2 * j:2 * j + 2], lhsT=wt[:, kc, jj * 128:(jj + 1) * 128], rhs=sc[:, kc, :], start=(kc == 0), stop=(kc == 7))
                    return ins
                S.op("pe", mm, reads=[rw, r_sc], writes=[rp])
            S.op("dve", lambda e: e.tensor_tensor(out=modT[:], in0=ps[:, 0:96].rearrange("p (j t) -> p j t", t=2),
                                                  in1=bm[:].unsqueeze(2).to_broadcast([128, 48, 2]), op=ALU.add), reads=[rp, r_bm], writes=[R_mod])
            S.op("dve", lambda e: e.tensor_scalar_add(out=mod1p[:], in0=modT[:], scalar1=1.0), reads=[R_mod], writes=[R_mod])
            for i, src in enumerate((ln1_g, ln1_b, ln2_g, ln2_b)):
                S.dma("sp", vecs[:, 8 * i:8 * i + 8], src[l, :].rearrange("(j p) -> p j", p=128), R_vecs, writes=[R_vecs], **NCK)
            for h in range(2):
                S.dma("sp", vecs[64 * h:64 * h + 64, 32:33], gq_qn[l, :].rearrange("(p o) -> p o", o=1), R_vecs, writes=[R_vecs], **NCK)
                S.dma("sp", vecs[64 * h:64 * h + 64, 33:34], gq_kn[l, :].rearrange("(p o) -> p o", o=1), R_vecs, writes=[R_vecs], **NCK)
            S.dma("sp", vecs[:, 34:35], da_subln[l, :].rearrange("(p o) -> p o", o=1), R_vecs, writes=[R_vecs], **NCK)
            S.dma("sp", lamt[:].rearrange("p a d -> p (a d)"), lam4[l:l + 1, :, :].rearrange("o a d -> o (a d)").partition_broadcast(128), r_lam, writes=[r_lam])
            S.op("dve", lambda e: e.tensor_scalar_mul(out=vecs[:, 34:35], in0=vecs[:, 34:35], scalar1=float(1.0 - lam_init)), reads=[R_vecs], writes=[R_vecs])
            for t in range(2):
                S.op("dve", lambda e, t=t: e.tensor_tensor(out=ltmp[:, t, :], in0=lamt[:, 2 * t, :], in1=lamt[:, 2 * t + 1, :], op=ALU.mult), reads=[r_lam], writes=[r_lt])
                S.op("dve", lambda e, t=t: e.reduce_sum(out=vecs[:, 36 + t:37 + t], in_=ltmp[:, t, :], axis=AX.X), reads=[r_lt], writes=[R_vecs])
            S.op("act", lambda e: e.activation(out=vecs[:, 36:38], in_=vecs[:, 36:38], func=AF.Exp), reads=[R_vecs], writes=[R_vecs])
            S.op("dve", lambda e: e.tensor_tensor(out=vecs[:, 35:36], in0=vecs[:, 37:38], in1=vecs[:, 36:37], op=ALU.subtract), reads=[R_vecs], writes=[R_vecs])
            S.op("dve", lambda e: e.tensor_scalar_add(out=vecs[:, 35:36], in0=vecs[:, 35:36], scalar1=float(-lam_init)), reads=[R_vecs], writes=[R_vecs])
            for k in range(3):
                S.dma("sp", fcw[:, :, k], ffn_conv_w[l, k, :].rearrange("(j p) -> p j", p=128), R_vecs, writes=[R_vecs], **NCK)
                S.dma("sp", hcw[:, :, k], hy_conv_w[l, k, :].rearrange("(j p) -> p j", p=128), R_vecs, writes=[R_vecs], **NCK)
            S.dma("sp", fcw[:, :, 3], ffn_conv_b[l, :].rearrange("(j p) -> p j", p=128), R_vecs, writes=[R_vecs], **NCK)
            S.dma("sp", hcw[:, :, 3], hy_conv_b[l, :].rearrange("(j p) -> p j", p=128), R_vecs, writes=[R_vecs], **NCK)
        S.end_phase()

    def ln_stats(xs, n, rx, pst, pf, xb, r_xb, sq, r_sq):
        S.op("dve", lambda e: e.tensor_copy(out=xb[:, :, 0:n], in_=xs[:, :, 0:n]), reads=[rx], writes=[r_xb])
        S.op("act", lambda e: e.activation(out=sq[:, :, 0:n], in_=xs[:, :, 0:n], func=AF.Square), reads=[rx], writes=[r_sq])
        ps1, rp1 = pst.get(); ps2, rp2 = pst.get()

        def mm(pe, ps, src):
            for kc in range(8):
                ins = pe.matmul(ps[:, 0:n], lhsT=CB[:, ONES, :], rhs=src[:, kc, 0:n], start=(kc == 0), stop=(kc == 7))
            return ins
        S.op("pe", lambda pe: mm(pe, ps1, xb), reads=[r_xb, R_CB], writes=[rp1])
        S.op("pe", lambda pe: mm(pe, ps2, sq), reads=[r_sq, R_CB], writes=[rp2])
        mean, rm = pf.get(); msq, rq = pf.get(); rstd, rr = pf.get()
        S.op("dve", lambda e: e.tensor_scalar_mul(out=mean[:, 0:n], in0=ps1[:, 0:n], scalar1=1.0 / D), reads=[rp1], writes=[rm])
        S.op("dve", lambda e: e.tensor_tensor(out=msq[:, 0:n], in0=mean[:, 0:n], in1=mean[:, 0:n], op=ALU.mult), reads=[rm], writes=[rq])
        S.op("dve", lambda e: e.scalar_tensor_tensor(out=rstd[:, 0:n], in0=ps2[:, 0:n], scalar=1.0 / D, in1=msq[:, 0:n], op0=ALU.mult, op1=ALU.subtract),
             reads=[rp2, rq], writes=[rr])
        S.op("act", lambda e: e.activation(out=rstd[:, 0:n], in_=rstd[:, 0:n], func=AF.Sqrt, bias=CF[:, ZERO, 0:1], scale=1.0), reads=[rr, R_CB], writes=[rr])
        S.op("dve", lambda e: e.reciprocal(out=rstd[:, 0:n], in_=rstd[:, 0:n]), reads=[rr], writes=[rr])
        return mean, rm, rstd, rr

    def phase1(l, cur, last):
        with ExitStack() as st:
            S.pst = st
            wqk = sb("wqk", [128, 8, 1792], BF16, st); r_wqk = S.R("wqk")
            wv = sb("wv", [128, 8, 640], BF16, st); r_wv = S.R("wv")
            why = sb("why", [128, 8, 1536], BF16, st); r_why = S.R("why")

            def wsrc(c0, cw):
                return w_in_b[:, c0:c0 + cw].rearrange("(kc p) n -> p kc n", p=128)
            for i, (kind, c0) in enumerate(QK_SRC):
                if kind == "kb":
                    for h in range(2):
                        S.dma("sp", wqk[:, :, i * 128 + 64 * h:i * 128 + 64 * h + 64], wsrc(c0, 64), r_wqk, writes=[r_wqk])
                else:
                    S.dma("sp", wqk[:, :, i * 128:(i + 1) * 128], wsrc(c0, 128), r_wqk, writes=[r_wqk])
            S.dma("sp", wv[:, :, 0:512], wsrc(512, 512), r_wv, writes=[r_wv])
            S.dma("sp", wv[:, :, 512:640], wsrc(1152, 128), r_wv, writes=[r_wv])
            S.dma("sp", why[:], wsrc(2304, 1536), r_why, writes=[r_why])
            pxs = TPool(S, "xs", [128, 8, 512], F32, 2); phs = TPool(S, "hs", [128, 8, 512], BF16, 2)
            xb = sb("xb", [128, 8, 512], BF16, st); r_xb = S.R("xb"); sq = sb("sq", [128, 8, 512], BF16, st); r_sq = S.R("sq")
            pst = TPool(S, "ps1s", [128, 512], F32, 2, psum=True); pproj = TPool(S, "ps1p", [128, 512], F32, 3, psum=True)
            paux = TPool(S, "ps1a", [128, 512], F32, 2, psum=True)
            pf = TPool(S, "f1", [128, 512], F32, 10); pbw = TPool(S, "b1", [128, 512], BF16, 6)
            prc = TPool(S, "rc", [128, 2, 512], F32, 2)
            pu = TPool(S, "ucat", [128, 520], F32, 3)
            pvt = TPool(S, "vt", [128, 640], BF16, 2)
            carry = sb("carry", [128, 12, 3], F32, st); r_carry = S.R("carry")
            fl = sb("flush", [128, 12, 2], BF16, st); r_fl = S.R("flush")
            flt = sb("flusht", [128, 12, 4], F32, st); flo = sb("flusho", [128, 12, 2], F32, st)
            tiles = [(0, 256, 1)] + [(NC_ + 512 * i, 512, 0) for i in range(NL // 512)]
            S.op("pool", lambda e: e.memset(carry[:], 0.0), writes=[r_carry])
            for (u0, n, isc) in tiles:
                seq = isc; t0 = u0 if isc else u0 - NC_
                xs, rx = pxs.get()
                S.dma("sp", xs[:, :, 0:n], xT[cur][:, u0:u0 + n].rearrange("(fc p) n -> p fc n", p=128), rx, writes=[rx])
                if not isc:
                    rc, rrc = prc.get()
                    S.dma("sp", rc[:, 0, :], ropeC[:, u0:u0 + n], rrc, writes=[rrc])
                    S.dma("sp", rc[:, 1, :], ropeS[:, u0:u0 + n], rrc, writes=[rrc])
                mean, rm, rstd, rr = ln_stats(xs, n, rx, pst, pf, xb, r_xb, sq, r_sq)
                hs, rh = phs.get()
                for kc in range(8):
                    t1, r1 = pf.get()
                    S.op("dve", lambda e, kc=kc, t1=t1: e.tensor_tensor(out=t1[:, 0:n], in0=xs[:, kc, 0:n], in1=mean[:, 0:n], op=ALU.subtract), reads=[rx, rm], writes=[r1])
                    S.op("pool", lambda e, t1=t1: e.tensor_tensor(out=t1[:, 0:n], in0=t1[:, 0:n], in1=rstd[:, 0:n], op=ALU.mult), reads=[r1, rr], writes=[r1])
                    S.op("act", lambda e, kc=kc, t1=t1: e.activation(out=hs[:, kc, 0:n], in_=t1[:, 0:n], func=AF.Identity,
                                                                      scale=mod1p[:, 8 + kc, isc:isc + 1], bias=modT[:, kc, isc:isc + 1]), reads=[r1, R_mod], writes=[rh])
                S.dma("sp", hT[:, u0:u0 + n].rearrange("(fc p) n -> p fc n", p=128), hs[:, :, 0:n], rh, reads=[rh])

                def proj(ps, wt, c0, cw, rw):
                    def mm(pe):
                        for kc in range(8):
                            ins = pe.matmul(ps[0:cw, 0:n], lhsT=wt[:, kc, c0:c0 + cw], rhs=hs[:, kc, 0:n], start=(kc == 0), stop=(kc == 7))
                        return ins
                    return mm
                P1STOP = int(os.environ.get('P1STOP', '9'))
                if P1STOP >= 2:
                    for i, (kind, c0) in enumerate(QK_SRC):
                        if last and isc and kind in ("q", "qb"):
                            continue
                        ps, rp = pproj.get()
                        S.op("pe", proj(ps, wqk, i * 128, 128, r_wqk), reads=[rh, r_wqk], writes=[rp])
                        src = ps; rsrc = rp
                        P1MODE = int(os.environ.get("P1MODE", "2"))
                        if kind in ("kb", "qb") and P1MODE in (1, 2):
                            sqb, rsb = pbw.get()
                            S.op("act", lambda e, ps=ps, sqb=sqb: e.activation(out=sqb[:, 0:n], in_=ps[:, 0:n], func=AF.Square), reads=[rp], writes=[rsb])
                            psm, rpm = paux.get()
                            S.op("pe", lambda pe, psm=psm, sqb=sqb: pe.matmul(psm[:, 0:n], lhsT=CB[:, BLK64, :], rhs=sqb[:, 0:n], start=True, stop=True), reads=[rsb, R_CB], writes=[rpm])
                            r2, rr2 = pf.get()
                            S.op("act", lambda e, psm=psm, r2=r2: e.activation(out=r2[:, 0:n], in_=psm[:, 0:n], func=AF.Sqrt, bias=CF[:, ZERO, 0:1], scale=1.0), reads=[rpm, R_CB], writes=[rr2])
                            S.op("dve", lambda e, r2=r2: e.reciprocal(out=r2[:, 0:n], in_=r2[:, 0:n]), reads=[rr2], writes=[rr2])
                            xn, rxn = pf.get()
                            gcol = 33 if kind == "kb" else 32
                            S.op("dve", lambda e, ps=ps, xn=xn, r2=r2, gcol=gcol: e.scalar_tensor_tensor(out=xn[:, 0:n], in0=ps[:, 0:n], scalar=vecs[:, gcol:gcol + 1], in1=r2[:, 0:n], op0=ALU.mult, op1=ALU.mult),
                                 reads=[rp, rr2, R_vecs], writes=[rxn])
                            src = xn; rsrc = rxn
                        ob, rob = pbw.get()
                        if isc or P1MODE in (0, 1):
                            copy_any(ob[:, 0:n], src[:, 0:n], [rsrc], [rob])
                        else:
                            RP = int(os.environ.get("P1ROPE", "5"))
                            x16, r16 = pbw.get()
                            S.op("act", act_copy(x16[:, 0:n], src[:, 0:n]), reads=[rsrc], writes=[r16])
                            if RP >= 2:
                                psw, rpw = paux.get()
                                S.op("pe", lambda pe, psw=psw, x16=x16: pe.matmul(psw[:, 0:n], lhsT=CB[:, PERM, :], rhs=x16[:, 0:n], start=True, stop=True), reads=[r16, R_CB], writes=[rpw])
                            ta, rta = pf.get(); tb, rtb = pf.get()
                            if RP >= 3:
                                S.op("dve", lambda e, ta=ta, src=src: e.tensor_tensor(out=ta[:, 0:n], in0=src[:, 0:n], in1=rc[:, 0, 0:n], op=ALU.mult), reads=[rsrc, rrc, r16], writes=[rta])
                            if RP >= 4:
                                S.op("dve", lambda e, tb=tb, psw=psw: e.tensor_tensor(out=tb[:, 0:n], in0=psw[:, 0:n], in1=rc[:, 1, 0:n], op=ALU.mult), reads=[rpw, rrc], writes=[rtb])
                            if RP >= 5:
                                S.op("dve", lambda e, ta=ta, tb=tb, ob=ob: e.tensor_tensor(out=ob[:, 0:n], in0=ta[:, 0:n], in1=tb[:, 0:n], op=ALU.add), reads=[rta, rtb], writes=[rob])
                            else:
                                copy_any(ob[:, 0:n], src[:, 0:n], [rsrc, r16], [rob])
                        S.dma("sp", qkT[i * 128:(i + 1) * 128, u0:u0 + n], ob[:, 0:n], rob, reads=[rob])
                for sub in range(n // 128 if P1STOP >= 3 else 0):
                    psv, rpv = pproj.get(); psb, rpb = paux.get()

                    def mmv(pe, psv=psv, psb=psb, sub=sub):
                        for kc in range(8):
                            pe.matmul(psv[:, 0:512], lhsT=hs[:, kc, sub * 128:(sub + 1) * 128], rhs=wv[:, kc, 0:512], start=(kc == 0), stop=(kc == 7))
                        for kc in range(8):
                            ins = pe.matmul(psb[:, 0:128], lhsT=hs[:, kc, sub * 128:(sub + 1) * 128], rhs=wv[:, kc, 512:640], start=(kc == 0), stop=(kc == 7))
                        return ins
                    S.op("pe", mmv, reads=[rh, r_wv], writes=[rpv, rpb])
                    vt, rvt = pvt.get()
                    S.op("act", act_copy(vt[:, 0:512], psv[:, 0:512]), reads=[rpv], writes=[rvt])
                    S.op("dve", lambda e, vt=vt, psb=psb: e.tensor_copy(out=vt[:, 512:640], in_=psb[:, 0:128]), reads=[rpb], writes=[rvt])
                    S.dma("sp", vtok[u0 + sub * 128:u0 + (sub + 1) * 128, :], vt[:], rvt, reads=[rvt])
                if not (last and isc) and P1STOP >= 4:
                    for j in range(12):
                        ps, rp = pproj.get()
                        S.op("pe", proj(ps, why, j * 128, 128, r_why), reads=[rh, r_why], writes=[rp])
                        uc, ruc = pu.get()
                        S.op("pool", lambda e, uc=uc, j=j: e.tensor_copy(out=uc[:, 0:3], in_=carry[:, j, :]), reads=[r_carry], writes=[ruc])
                        S.op("act", act_copy(uc[:, 3:3 + n], ps[:, 0:n]), reads=[rp], writes=[ruc])
                        S.op("pool", lambda e, uc=uc, j=j: e.tensor_copy(out=carry[:, j, :], in_=uc[:, n:n + 3]), reads=[ruc], writes=[r_carry])
                        ya, rya = pf.get(); yb, ryb = pbw.get()
                        S.op("act", lambda e, uc=uc, ya=ya, j=j: e.activation(out=ya[:, 0:n], in_=uc[:, 1:1 + n], func=AF.Identity, scale=hcw[:, j, 1:2], bias=hcw[:, j, 3:4]), reads=[ruc, R_vecs], writes=[rya])
                        S.op("dve", lambda e, uc=uc, ya=ya, j=j: e.scalar_tensor_tensor(out=ya[:, 0:n], in0=uc[:, 0:n], scalar=hcw[:, j, 0:1], in1=ya[:, 0:n], op0=ALU.mult, op1=ALU.add), reads=[ruc, rya, R_vecs], writes=[rya])
                        S.op("dve", lambda e, uc=uc, ya=ya, yb=yb, j=j: e.scalar_tensor_tensor(out=yb[:, 0:n], in0=uc[:, 2:2 + n], scalar=hcw[:, j, 2:3], in1=ya[:, 0:n], op0=ALU.mult, op1=ALU.add), reads=[ruc, rya, R_vecs], writes=[ryb])
                        if t0 == 0:
                            S.dma("sp", hyT[seq, j * 128:(j + 1) * 128, 0:n - 2], yb[:, 2:n], ryb, reads=[ryb])
                        else:
                            S.dma("sp", hyT[seq, j * 128:(j + 1) * 128, t0 - 2:t0 + n - 2], yb[:, 0:n], ryb, reads=[ryb])
                    T_end = NC_ if isc else NL
                    if t0 + n == T_end:
                        S.op("pool", lambda e: e.memset(flt[:], 0.0), writes=[r_fl])
                        S.op("pool", lambda e: e.tensor_copy(out=flt[:, :, 0:3], in_=carry[:]), reads=[r_carry], writes=[r_fl])
                        for j in range(12):
                            S.op("act", lambda e, j=j: e.activation(out=flo[:, j, :], in_=flt[:, j, 1:3], func=AF.Identity, scale=hcw[:, j, 1:2], bias=hcw[:, j, 3:4]), reads=[r_fl, R_vecs], writes=[r_fl])
                            S.op("dve", lambda e, j=j: e.scalar_tensor_tensor(out=flo[:, j, :], in0=flt[:, j, 0:2], scalar=hcw[:, j, 0:1], in1=flo[:, j, :], op0=ALU.mult, op1=ALU.add), reads=[r_fl, R_vecs], writes=[r_fl])
                            S.op("dve", lambda e, j=j: e.scalar_tensor_tensor(out=fl[:, j, :], in0=flt[:, j, 2:4], scalar=hcw[:, j, 2:3], in1=flo[:, j, :], op0=ALU.mult, op1=ALU.add), reads=[r_fl, R_vecs], writes=[r_fl])
                        S.dma("sp", hyT[seq, :, T_end - 2:T_end].rearrange("(j p) t -> p j t", p=128), fl[:], r_fl, reads=[r_fl], **NCK)
                        S.op("pool", lambda e: e.memset(carry[:], 0.0), reads=[r_carry], writes=[r_carry])
        S.end_phase()

    def phase_attn(l, last):
        with ExitStack() as st:
            S.pst = st
            pq = TPool(S, "aq", [128, U], BF16, 2); pk = TPool(S, "ak", [128, U], BF16, 2); pv = TPool(S, "av", [128, 66, 128], BF16, 2)
            pss = TPool(S, "as", [128, 512], F32, 3, psum=True); pacc = TPool(S, "aa", [128, 512], F32, 4, psum=True)
            paux = TPool(S, "ax", [128, 512], F32, 1, psum=True)
            pp = TPool(S, "ap", [128, 512], BF16, 6); pf = TPool(S, "af", [128, 512], F32, 8); pob = TPool(S, "aob", [128, 512], BF16, 3)
            passes = [("d", h, 4 + h, h * 128, 128, h * 128) for h in range(4)] + \
                     [("g", 8 + c // 2, 10 + c, 512 + (c // 2) * 64, 64, 512 + c * 128) for c in range(4)]
            qtiles = ([] if last else [(0, 256, 2)]) + [(NC_ + 512 * i, 512, 66) for i in range(NL // 512)]
            for (kind, kch, qch, vc0, dv, orow) in passes:
                QT, rq = pq.get(); KT, rk = pk.get(); V, rv = pv.get()
                S.dma("sp", QT[:], qkT[qch * 128:(qch + 1) * 128, :], rq, writes=[rq])
                S.dma("sp", KT[:], qkT[kch * 128:(kch + 1) * 128, :], rk, writes=[rk])
                S.dma("sp", V[:, :, 0:dv], vtok[:, vc0:vc0 + dv].rearrange("(c p) d -> p c d", p=128), rv, writes=[rv])
                for (q0, n, nkc) in qtiles:
                    acc = [pacc.get() for _ in range(4)]
                    for kc in range(nkc):
                        for h in range(2):
                            ps, rp = pss.get()
                            S.op("pe", lambda pe, ps=ps, h=h, kc=kc: pe.matmul(ps[:, 0:n], lhsT=KT[64 * h:64 * h + 64, kc * 128:(kc + 1) * 128],
                                                                              rhs=QT[64 * h:64 * h + 64, q0:q0 + n], start=True, stop=True), reads=[rk, rq], writes=[rp])
                            p, rpp = pp.get()
                            S.op("act", lambda e, ps=ps, p=p: e.activation(out=p[:, 0:n], in_=ps[:, 0:n], func=AF.Exp, scale=0.125), reads=[rp], writes=[rpp])
                            (o, ro), (lt, rl) = acc[2 * h], acc[2 * h + 1]

                            def pvmm(pe, o=o, lt=lt, p=p, kc=kc):
                                pe.matmul(o[0:dv, 0:n], lhsT=V[:, kc, 0:dv], rhs=p[:, 0:n], start=(kc == 0), stop=(kc == nkc - 1))
                                return pe.matmul(lt[0:dv, 0:n], lhsT=CB[:, ONES, 0:dv], rhs=p[:, 0:n], start=(kc == 0), stop=(kc == nkc - 1))
                            S.op("pe", pvmm, reads=[rv, rpp, R_CB], writes=[ro, rl])
                    tn = []
                    for h in range(2):
                        (o, ro), (lt, rl) = acc[2 * h], acc[2 * h + 1]
                        r, rr = pf.get(); t, rt = pf.get()
                        S.op("dve", lambda e, r=r, lt=lt: e.reciprocal(out=r[0:dv, 0:n], in_=lt[0:dv, 0:n]), reads=[rl], writes=[rr])
                        S.op("dve", lambda e, r=r, t=t, o=o: e.tensor_tensor(out=t[0:dv, 0:n], in0=o[0:dv, 0:n], in1=r[0:dv, 0:n], op=ALU.mult), reads=[ro, rr], writes=[rt])
                        tn.append((t, rt))
                    if kind == "d":
                        (t1, rt1), (t2, rt2) = tn
                        dd, rd = pf.get()
                        S.op("dve", lambda e, dd=dd, t1=t1, t2=t2: e.scalar_tensor_tensor(out=dd[:, 0:n], in0=t2[:, 0:n], scalar=vecs[:, 35:36], in1=t1[:, 0:n], op0=ALU.mult, op1=ALU.add),
                             reads=[rt1, rt2, R_vecs], writes=[rd])
                        sqb, rsb = pp.get()
                        S.op("act", lambda e, dd=dd, sqb=sqb: e.activation(out=sqb[:, 0:n], in_=dd[:, 0:n], func=AF.Square), reads=[rd], writes=[rsb])
                        psm, rpm = paux.get()
                        S.op("pe", lambda pe, psm=psm, sqb=sqb: pe.matmul(psm[:, 0:n], lhsT=CB[:, ONES128, :], rhs=sqb[:, 0:n], start=True, stop=True), reads=[rsb, R_CB], writes=[rpm])
                        r2, rr2 = pf.get()
                        S.op("act", lambda e, psm=psm, r2=r2: e.activation(out=r2[:, 0:n], in_=psm[:, 0:n], func=AF.Sqrt, bias=CF[:, ZERO, 0:1], scale=1.0), reads=[rpm, R_CB], writes=[rr2])
                        S.op("dve", lambda e, r2=r2: e.reciprocal(out=r2[:, 0:n], in_=r2[:, 0:n]), reads=[rr2], writes=[rr2])
                        ob, rob = pob.get()
                        S.op("dve", lambda e, ob=ob, dd=dd, r2=r2: e.scalar_tensor_tensor(out=ob[:, 0:n], in0=dd[:, 0:n], scalar=vecs[:, 34:35], in1=r2[:, 0:n], op0=ALU.mult, op1=ALU.mult),
                             reads=[rd, rr2, R_vecs], writes=[rob])
                        S.dma("sp", abcT[orow:orow + 128, q0:q0 + n], ob[:, 0:n], rob, reads=[rob])
                    else:
                        for h in range(2):
                            t, rt = tn[h]
                            ob, rob = pob.get()
                            S.op("dve", lambda e, ob=ob, t=t: e.tensor_copy(out=ob[0:64, 0:n], in_=t[0:64, 0:n]), reads=[rt], writes=[rob])
                            S.dma("sp", abcT[orow + 64 * h:orow + 64 * h + 64, q0:q0 + n], ob[0:64, 0:n], rob, reads=[rob])
        S.end_phase()

    def phase3a(l, cur, last):
        with ExitStack() as st:
            S.pst = st
            wg = sb("wg", [128, 8, 3072], BF16, st); r_wg = S.R("wg")
            wp = sb("wp", [128, 12, 1024], BF16, st); r_wp = S.R("wp")
            wo = sb("wo", [128, 8, 1024], BF16, st); r_wo = S.R("wo")
            for b in range(3):
                S.dma("sp", wg[:, :, b * 1024:(b + 1) * 1024], w_in_b[:, 3840 + b * 1024:3840 + (b + 1) * 1024].rearrange("(kc p) n -> p kc n", p=128), r_wg, writes=[r_wg])
            S.dma("sp", wp[:], w_pabc_b[:, :].rearrange("(kc p) n -> p kc n", p=128), r_wp, writes=[r_wp])
            S.dma("sp", wo[:], w_o_b[:, :].rearrange("(kc p) n -> p kc n", p=128), r_wo, writes=[r_wo])
            phs = TPool(S, "3hs", [128, 8, 512], BF16, 1); pab = TPool(S, "3ab", [128, 12, 512], BF16, 1); pxs = TPool(S, "3xs", [128, 8, 512], F32, 1)
            mT = sb("3mT", [128, 8, 512], BF16, st); r_mT = S.R("3mT")
            xb = sb("3xb", [128, 8, 512], BF16, st); r_xb = S.R("3xb"); sq = sb("3sq", [128, 8, 512], BF16, st); r_sq = S.R("3sq")
            pm = TPool(S, "3pm", [128, 512], F32, 4, psum=True); po = TPool(S, "3po", [128, 512], F32, 2, psum=True); pst = TPool(S, "3ps", [128, 512], F32, 2, psum=True)
            pf = TPool(S, "3f", [128, 512], F32, 10)
            tiles = ([] if last else [(0, 256, 1)]) + [(NC_ + 512 * i, 512, 0) for i in range(NL // 512)]
            for (u0, n, isc) in tiles:
                hs, rh = phs.get(); ab, ra = pab.get(); xs, rx = pxs.get()
                S.dma("sp", hs[:, :, 0:n], hT[:, u0:u0 + n].rearrange("(fc p) n -> p fc n", p=128), rh, writes=[rh])
                S.dma("sp", ab[:, :, 0:n], abcT[:, u0:u0 + n].rearrange("(fc p) n -> p fc n", p=128), ra, writes=[ra])
                S.dma("sp", xs[:, :, 0:n], xT[cur][:, u0:u0 + n].rearrange("(fc p) n -> p fc n", p=128), rx, writes=[rx])
                for fo in range(8):
                    m, rmm = pf.get()
                    for b in range(3):
                        psP, rpP = pm.get(); psG, rpG = pm.get()

                        def mmP(pe, psP=psP, b=b, fo=fo):
                            for k in range(4):
                                ins = pe.matmul(psP[:, 0:n], lhsT=wp[:, 4 * b + k, fo * 128:(fo + 1) * 128], rhs=ab[:, 4 * b + k, 0:n], start=(k == 0), stop=(k == 3))
                            return ins

                        def mmG(pe, psG=psG, b=b, fo=fo):
                            for kc in range(8):
                                ins = pe.matmul(psG[:, 0:n], lhsT=wg[:, kc, b * 1024 + fo * 128:b * 1024 + (fo + 1) * 128], rhs=hs[:, kc, 0:n], start=(kc == 0), stop=(kc == 7))
                            return ins
                        S.op("pe", mmP, reads=[ra, r_wp], writes=[rpP])
                        S.op("pe", mmG, reads=[rh, r_wg], writes=[rpG])
                        sg, rsg = pf.get()
                        S.op("act", lambda e, sg=sg, psG=psG: e.activation(out=sg[:, 0:n], in_=psG[:, 0:n], func=AF.Sigmoid), reads=[rpG], writes=[rsg])
                        if b == 0:
                            S.op("dve", lambda e, m=m, psP=psP, sg=sg: e.tensor_tensor(out=m[:, 0:n], in0=psP[:, 0:n], in1=sg[:, 0:n], op=ALU.mult), reads=[rpP, rsg], writes=[rmm])
                        else:
                            S.op("dve", lambda e, psP=psP, sg=sg: e.tensor_tensor(out=sg[:, 0:n], in0=psP[:, 0:n], in1=sg[:, 0:n], op=ALU.mult), reads=[rpP, rsg], writes=[rsg])
                            if b == 1:
                                S.op("pool", lambda e, m=m, sg=sg: e.tensor_tensor(out=m[:, 0:n], in0=m[:, 0:n], in1=sg[:, 0:n], op=ALU.add), reads=[rmm, rsg], writes=[rmm])
                            else:
                                S.op("dve", lambda e, m=m, sg=sg, fo=fo: e.tensor_tensor(out=mT[:, fo, 0:n], in0=m[:, 0:n], in1=sg[:, 0:n], op=ALU.add), reads=[rmm, rsg], writes=[r_mT])
                for fo in range(8):
                    pso, rpo = po.get()

                    def mmO(pe, pso=pso, fo=fo):
                        for k in range(8):
                            ins = pe.matmul(pso[:, 0:n], lhsT=wo[:, k, fo * 128:(fo + 1) * 128], rhs=mT[:, k, 0:n], start=(k == 0), stop=(k == 7))
                        return ins
                    S.op("pe", mmO, reads=[r_mT, r_wo], writes=[rpo])
                    t, rt = pf.get()
                    S.op("act", lambda e, t=t, pso=pso, fo=fo: e.activation(out=t[:, 0:n], in_=pso[:, 0:n], func=AF.Identity, scale=modT[:, 16 + fo, isc:isc + 1]), reads=[rpo, R_mod], writes=[rt])
                    S.op("dve", lambda e, t=t, fo=fo: e.scalar_tensor_tensor(out=xs[:, fo, 0:n], in0=xs[:, fo, 0:n], scalar=float(ALPHA), in1=t[:, 0:n], op0=ALU.mult, op1=ALU.add), reads=[rx, rt], writes=[rx])
                mean, rm, rstd, rr = ln_stats(xs, n, rx, pst, pf, xb, r_xb, sq, r_sq)
                for kc in range(8):
                    t1, r1 = pf.get()
                    S.op("dve", lambda e, kc=kc, t1=t1: e.tensor_tensor(out=t1[:, 0:n], in0=xs[:, kc, 0:n], in1=mean[:, 0:n], op=ALU.subtract), reads=[rx, rm], writes=[r1])
                    S.op("pool", lambda e, t1=t1: e.tensor_tensor(out=t1[:, 0:n], in0=t1[:, 0:n], in1=rstd[:, 0:n], op=ALU.mult), reads=[r1, rr], writes=[r1])
                    S.op("act", lambda e, kc=kc, t1=t1: e.activation(out=xs[:, kc, 0:n], in_=t1[:, 0:n], func=AF.Identity, scale=vecs[:, kc:kc + 1], bias=vecs[:, 8 + kc:9 + kc]), reads=[r1, R_vecs], writes=[rx])
                S.dma("sp", x1T[:, u0:u0 + n].rearrange("(fc p) n -> p fc n", p=128), xs[:, :, 0:n], rx, reads=[rx])
        S.end_phase()

    def phase3b(l, cur, last):
        nxt = cur ^ 1
        with ExitStack() as st:
            S.pst = st
            wd = sb("wd", [128, 22, 1024], BF16, st); r_wd = S.R("wd")
            S.dma("sp", wd[:], w_down_b[:, :].rearrange("(kc p) n -> p kc n", p=128), r_wd, writes=[r_wd])
            pwu = TPool(S, "wu", [128, 2, 8, 128], BF16, 3)
            pxs = TPool(S, "bxs", [128, 8, 512], F32, 2)
            h2 = sb("bh2", [128, 8, 512], BF16, st); r_h2 = S.R("bh2")
            aT = sb("baT", [128, 22, 512], BF16, st); r_aT = S.R("baT")
            xb = sb("bxb", [128, 8, 512], BF16, st); r_xb = S.R("bxb"); sq = sb("bsq", [128, 8, 512], BF16, st); r_sq = S.R("bsq")
            ys = sb("bys", [128, 8, 512], F32, st); r_ys = S.R("bys")
            carry = sb("bcarry", [128, 44, 2], F32, st); r_carry = S.R("bcarry")
            xcar = sb("bxcar", [128, 8, 1], F32, st); r_xcar = S.R("bxcar")
            pu = TPool(S, "bu", [128, 520], F32, 4); pf = TPool(S, "bf", [128, 512], F32, 10)
            pup = TPool(S, "bpu", [128, 512], F32, 3, psum=True); pdn = TPool(S, "bpd", [128, 512], F32, 2, psum=True); pst = TPool(S, "bps", [128, 512], F32, 2, psum=True)
            tiles = ([] if last else [(0, 256, 1)]) + [(NC_ + 512 * i, 512, 0) for i in range(NL // 512)]
            S.op("pool", lambda e: e.memset(carry[:], 0.0), writes=[r_carry])
            S.op("pool", lambda e: e.memset(xcar[:], 0.0), writes=[r_xcar])

            def conv_chunk(jj, n, ps, rp, flush):
                uc, ruc = pu.get()
                S.op("pool", lambda e: e.tensor_copy(out=uc[:, 0:2], in_=carry[:, jj, :]), reads=[r_carry], writes=[ruc])
                if flush:
                    S.op("pool", lambda e: e.memset(uc[:, 2:2 + n], 0.0), writes=[ruc])
                else:
                    S.op("act", act_copy(uc[:, 2:2 + n], ps[:, 0:n]), reads=[rp], writes=[ruc])
                    S.op("pool", lambda e: e.tensor_copy(out=carry[:, jj, :], in_=uc[:, n:n + 2]), reads=[ruc], writes=[r_carry])
                y, ry = pf.get()
                S.op("act", lambda e: e.activation(out=y[:, 0:n], in_=uc[:, 1:1 + n], func=AF.Identity, scale=fcw[:, jj, 1:2], bias=fcw[:, jj, 3:4]), reads=[ruc, R_vecs], writes=[ry])
                S.op("dve", lambda e: e.scalar_tensor_tensor(out=y[:, 0:n], in0=uc[:, 0:n], scalar=fcw[:, jj, 0:1], in1=y[:, 0:n], op0=ALU.mult, op1=ALU.add), reads=[ruc, ry, R_vecs], writes=[ry])
                S.op("dve", lambda e: e.scalar_tensor_tensor(out=y[:, 0:n], in0=uc[:, 2:2 + n], scalar=fcw[:, jj, 2:3], in1=y[:, 0:n], op0=ALU.mult, op1=ALU.add), reads=[ruc, ry, R_vecs], writes=[ry])
                return y, ry

            def tile(u0, n, isc, xs, rx, flush):
                if not flush:
                    mean, rm, rstd, rr = ln_stats(xs, n, rx, pst, pf, xb, r_xb, sq, r_sq)
                    for kc in range(8):
                        t1, r1 = pf.get()
                        S.op("dve", lambda e, kc=kc, t1=t1: e.tensor_tensor(out=t1[:, 0:n], in0=xs[:, kc, 0:n], in1=mean[:, 0:n], op=ALU.subtract), reads=[rx, rm], writes=[r1])
                        S.op("pool", lambda e, t1=t1: e.tensor_tensor(out=t1[:, 0:n], in0=t1[:, 0:n], in1=rstd[:, 0:n], op=ALU.mult), reads=[r1, rr], writes=[r1])
                        S.op("act", lambda e, kc=kc, t1=t1: e.activation(out=h2[:, kc, 0:n], in_=t1[:, 0:n], func=AF.Identity, scale=mod1p[:, 32 + kc, isc:isc + 1], bias=modT[:, 24 + kc, isc:isc + 1]), reads=[r1, R_mod], writes=[r_h2])
                for j in range(22):
                    ys2 = []
                    if not flush:
                        wu, rwu = pwu.get()
                        for t in range(2):
                            S.dma("sp", wu[:, t, :, :], w_up_b[:, t * DFF + j * 128:t * DFF + (j + 1) * 128].rearrange("(kc p) n -> p kc n", p=128), rwu, writes=[rwu])
                    for t in range(2):
                        ps = rp = None
                        if not flush:
                            ps, rp = pup.get()

                            def mm(pe, ps=ps, t=t):
                                for kc in range(8):
                                    ins = pe.matmul(ps[:, 0:n], lhsT=wu[:, t, kc, :], rhs=h2[:, kc, 0:n], start=(kc == 0), stop=(kc == 7))
                                return ins
                            S.op("pe", mm, reads=[rwu, r_h2], writes=[rp])
                        ys2.append(conv_chunk(t * 22 + j, n, ps, rp, flush))
                    (yv, ryv), (yg, ryg) = ys2
                    S.op("act", lambda e, yg=yg: e.activation(out=yg[:, 0:n], in_=yg[:, 0:n], func=AF.Silu), reads=[ryg], writes=[ryg])
                    S.op("dve", lambda e, yv=yv, yg=yg, j=j: e.tensor_tensor(out=aT[:, j, 0:n], in0=yv[:, 0:n], in1=yg[:, 0:n], op=ALU.mult), reads=[ryv, ryg], writes=[r_aT])
                for fo in range(8):
                    psd, rpd = pdn.get()

                    def mmd(pe, psd=psd, fo=fo):
                        for k in range(22):
                            ins = pe.matmul(psd[:, 0:n], lhsT=wd[:, k, fo * 128:(fo + 1) * 128], rhs=aT[:, k, 0:n], start=(k == 0), stop=(k == 21))
                        return ins
                    S.op("pe", mmd, reads=[r_aT, r_wd], writes=[rpd])
                    t, rt = pf.get()
                    S.op("act", lambda e, t=t, psd=psd, fo=fo: e.activation(out=t[:, 0:n], in_=psd[:, 0:n], func=AF.Identity, scale=modT[:, 40 + fo, isc:isc + 1]), reads=[rpd, R_mod], writes=[rt])
                    S.op("dve", lambda e, t=t, fo=fo: e.scalar_tensor_tensor(out=ys[:, fo, 0:1], in0=xcar[:, fo, 0:1], scalar=float(ALPHA), in1=t[:, 0:1], op0=ALU.mult, op1=ALU.add), reads=[r_xcar, rt], writes=[r_ys])
                    if n > 1:
                        S.op("dve", lambda e, t=t, fo=fo: e.scalar_tensor_tensor(out=ys[:, fo, 1:n], in0=xs[:, fo, 0:n - 1], scalar=float(ALPHA), in1=t[:, 1:n], op0=ALU.mult, op1=ALU.add), reads=[rx, rt], writes=[r_ys])
                    if not flush:
                        S.op("pool", lambda e, fo=fo: e.tensor_copy(out=xcar[:, fo, 0:1], in_=xs[:, fo, n - 1:n]), reads=[rx, r_ys], writes=[r_xcar])
                mean, rm, rstd, rr = ln_stats(ys, n, r_ys, pst, pf, xb, r_xb, sq, r_sq)
                for kc in range(8):
                    t1, r1 = pf.get()
                    S.op("dve", lambda e, kc=kc, t1=t1: e.tensor_tensor(out=t1[:, 0:n], in0=ys[:, kc, 0:n], in1=mean[:, 0:n], op=ALU.subtract), reads=[r_ys, rm], writes=[r1])
                    S.op("pool", lambda e, t1=t1: e.tensor_tensor(out=t1[:, 0:n], in0=t1[:, 0:n], in1=rstd[:, 0:n], op=ALU.mult), reads=[r1, rr], writes=[r1])
                    S.op("act", lambda e, kc=kc, t1=t1: e.activation(out=ys[:, kc, 0:n], in_=t1[:, 0:n], func=AF.Identity, scale=vecs[:, 16 + kc:17 + kc], bias=vecs[:, 24 + kc:25 + kc]), reads=[r1, R_vecs], writes=[r_ys])
                first = (u0 == 0) or (u0 == NC_)
                c0 = 1 if (first and not flush) else 0
                if n - c0 > 0:
                    S.dma("sp", xT[nxt][:, u0 - 1 + c0:u0 - 1 + n].rearrange("(fc p) n -> p fc n", p=128), ys[:, :, c0:n], r_ys, reads=[r_ys], **NCK)

            for (u0, n, isc) in tiles:
                xs, rx = pxs.get()
                S.dma("sp", xs[:, :, 0:n], x1T[:, u0:u0 + n].rearrange("(fc p) n -> p fc n", p=128), rx, writes=[rx])
                tile(u0, n, isc, xs, rx, False)
                end = (u0 + n == NC_) or (u0 + n == U)
                if end:
                    tile(u0 + n, 1, isc, xs, rx, True)
                    S.op("pool", lambda e: e.memset(carry[:], 0.0), reads=[r_carry], writes=[r_carry])
                    S.op("pool", lambda e: e.memset(xcar[:], 0.0), reads=[r_xcar], writes=[r_xcar])
        S.end_phase()

    TWO_PI = 2.0 * math.pi

    def phase_hyA(l, seqs):
        with ExitStack() as st:
            S.pst = st
            w1 = sb("hw1", [33, 64], F32, st); w2 = sb("hw2", [64, 64], F32, st); w3 = sb("hw3", [64, 2048], F32, st)
            hv = sb("hhv", [64, 4], F32, st); hb = sb("hhb", [128, 2, 4], F32, st); r_w = S.R("hw")
            S.dma("sp", w1[:], hy_w1[l, :, :], r_w, writes=[r_w]); S.dma("sp", w2[:], hy_w2[l, :, :], r_w, writes=[r_w])
            S.dma("sp", w3[:], hy_w3[l, :, :], r_w, writes=[r_w])
            for i, src in enumerate((hy_b1, hy_b2, hy_freq)):
                S.dma("sp", hv[:, i:i + 1], src[l, :].rearrange("(p o) -> p o", o=1), r_w, writes=[r_w], **NCK)
            for o in range(2):
                S.dma("sp", hb[:, o, :], hy_bias[l, o, :].rearrange("(c p) -> p c", p=128), r_w, writes=[r_w], **NCK)
            hid = [sb("hhid%d" % d, [64, NL], F32, st) for d in range(2)]; r_hid = [S.R("hhid%d" % d) for d in range(2)]
            gd = [sb("hgd%d" % d, [128, NL], F32, st) for d in range(2)]; r_gd = [S.R("hgd%d" % d) for d in range(2)]
            gb = [sb("hgb%d" % d, [128, NL], BF16, st) for d in range(2)]; r_gb = [S.R("hgb%d" % d) for d in range(2)]
            pz = TPool(S, "hz", [33, 512], F32, 2); pdt = TPool(S, "hdt", [128, 512], F32, 3); pt = TPool(S, "ht", [64, 512], F32, 5)
            pps = TPool(S, "hps", [128, 512], F32, 4, psum=True)
            sm = sb("hsm", [128, 8], F32, st); r_sm = S.R("hsm")

            def sin_layer(ps, rp, bcol, out_ap, r_out):
                t, rt = pt.get()
                S.op("dve", lambda e: e.tensor_scalar(out=t[:], in0=ps[0:64, :], scalar1=hv[:, bcol:bcol + 1], scalar2=hv[:, 2:3], op0=ALU.add, op1=ALU.mult), reads=[rp, r_w], writes=[rt])
                t2, rt2 = pt.get()
                MAGIC = 12582912.0
                S.op("dve", lambda e: e.tensor_scalar(out=t2[:], in0=t[:], scalar1=1.0 / TWO_PI, scalar2=MAGIC, op0=ALU.mult, op1=ALU.add), reads=[rt], writes=[rt2])
                S.op("dve", lambda e: e.tensor_scalar(out=t2[:], in0=t2[:], scalar1=-MAGIC, scalar2=-TWO_PI, op0=ALU.add, op1=ALU.mult), reads=[rt2], writes=[rt2])
                S.op("dve", lambda e: e.tensor_tensor(out=t[:], in0=t[:], in1=t2[:], op=ALU.add), reads=[rt, rt2], writes=[rt])
                S.op("act", lambda e: e.activation(out=out_ap, in_=t[:], func=AF.Sin), reads=[rt], writes=[r_out])

            for seq in seqs:
                for d in range(2):
                    for ptile in range(NL // 512):
                        z, rz = pz.get()
                        S.dma("sp", z[:], zT[seq, d, :, ptile * 512:(ptile + 1) * 512], rz, writes=[rz])
                        ps, rp = pps.get()
                        S.op("pe", lambda pe: pe.matmul(ps[0:64, :], lhsT=w1[:, :], rhs=z[:, :], start=True, stop=True), reads=[rz, r_w], writes=[rp])
                        h1, rh1 = pt.get()
                        sin_layer(ps, rp, 0, h1[:], rh1)
                        ps2, rp2 = pps.get()
                        S.op("pe", lambda pe: pe.matmul(ps2[0:64, :], lhsT=w2[:, :], rhs=h1[:, :], start=True, stop=True), reads=[rh1, r_w], writes=[rp2])
                        sin_layer(ps2, rp2, 1, hid[d][:, ptile * 512:(ptile + 1) * 512], r_hid[d])
                for o in range(2):
                    for cc in range(4):
                        for d in range(2):
                            col0 = o * 1024 + d * 512 + cc * 128
                            for ptile in range(NL // 512):
                                sl = slice(ptile * 512, (ptile + 1) * 512)
                                dt, rdt = pdt.get()
                                S.dma("sp", dt[:], dec[seq, d, cc * 128:(cc + 1) * 128, sl], rdt, writes=[rdt])
                                ps, rp = pps.get()
                                S.op("pe", lambda pe: pe.matmul(ps[:, :], lhsT=w3[:, col0:col0 + 128], rhs=hid[d][:, sl], start=True, stop=True), reads=[r_hid[d], r_w], writes=[rp])
                                S.op("dve", lambda e: e.tensor_tensor(out=gd[d][:, sl], in0=ps[:, :], in1=dt[:], op=ALU.mult), reads=[rp, rdt], writes=[r_gd[d]])
                            S.op("dve", lambda e: e.tensor_reduce(out=sm[:, d:d + 1], in_=gd[d][:], axis=AX.X, op=ALU.add, apply_absolute_value=True), reads=[r_gd[d]], writes=[r_sm])
                        S.op("dve", lambda e: e.tensor_tensor(out=sm[:, 2:3], in0=sm[:, 0:1], in1=sm[:, 1:2], op=ALU.add), reads=[r_sm], writes=[r_sm])
                        S.op("dve", lambda e: e.reciprocal(out=sm[:, 3:4], in_=sm[:, 2:3]), reads=[r_sm], writes=[r_sm])
                        for d in range(2):
                            eng = "dve"
                            S.op(eng, lambda e: e.tensor_scalar_mul(out=gb[d][:], in0=gd[d][:], scalar1=sm[:, 3:4]), reads=[r_gd[d], r_sm], writes=[r_gb[d]])
                        S.op("dve", lambda e: e.scalar_tensor_tensor(out=gb[0][:, 0:1], in0=gd[0][:, 0:1], scalar=sm[:, 3:4], in1=hb[:, o, cc:cc + 1], op0=ALU.mult, op1=ALU.add), reads=[r_gd[0], r_sm, r_w], writes=[r_gb[0]])
                        S.op("pool", lambda e: e.memset(gb[1][:, 0:1], 0.0), writes=[r_gb[1]])
                        for d in range(2):
                            S.dma("sp", gT[seq, o, cc * 128:(cc + 1) * 128, d * NL:(d + 1) * NL], gb[d][:], r_gb[d], reads=[r_gb[d]])
        S.end_phase()

    F1RHS = CB[:, 4:6, :].rearrange("p a b -> p (a b)")
    I1A = CB[:, 11:13, :].rearrange("p a b -> p (a b)")
    I1B = CB[:, 13:15, :].rearrange("p a b -> p (a b)")

    def fft_fwd(src, rsrc, K, cl0, psA, rpA, psX, rpX, pm12, pbq):
        def f1(pe):
            for c in range(4):
                ins = pe.matmul(psA[:, c * 256:(c + 1) * 256], lhsT=src[0:K, cl0 + c, :], rhs=F1RHS[0:K, :], start=True, stop=True)
            return ins
        S.op("pe", f1, reads=[rsrc, R_CB], writes=[rpA])
        bq, rbq = twiddle(psA, rpA, TWR, TWI, pm12, pbq)

        def f2(pe):
            pe.matmul(psX[:, 0:512], lhsT=CB[:, FR, :], rhs=bq[:, 0, :, :].rearrange("p c k -> p (c k)"), start=True, stop=False)
            pe.matmul(psX[:, 0:512], lhsT=CB[:, NFI, :], rhs=bq[:, 1, :, :].rearrange("p c k -> p (c k)"), start=False, stop=True)
            pe.matmul(psX[:, 512:1024], lhsT=CB[:, FI, :], rhs=bq[:, 0, :, :].rearrange("p c k -> p (c k)"), start=True, stop=False)
            return pe.matmul(psX[:, 512:1024], lhsT=CB[:, FR, :], rhs=bq[:, 1, :, :].rearrange("p c k -> p (c k)"), start=False, stop=True)
        S.op("pe", f2, reads=[rbq, R_CB], writes=[rpX])

    def twiddle(psA, rpA, tr, ti, pm12, pbq):
        m1, rm1 = pm12.get(); m2, rm2 = pm12.get()
        A3 = psA[:, :].rearrange("p (g k) -> p g k", k=128)
        S.op("dve", lambda e: e.tensor_tensor(out=m1[:], in0=A3, in1=CF[:, tr, :].unsqueeze(1).to_broadcast([128, 8, 128]), op=ALU.mult), reads=[rpA, R_CB], writes=[rm1])
        S.op("dve", lambda e: e.tensor_tensor(out=m2[:], in0=A3, in1=CF[:, ti, :].unsqueeze(1).to_broadcast([128, 8, 128]), op=ALU.mult), reads=[rpA, R_CB], writes=[rm2])
        bq, rbq = pbq.get()
        m1v = m1[:].rearrange("p (c r) k -> p c r k", r=2); m2v = m2[:].rearrange("p (c r) k -> p c r k", r=2)
        S.op("dve", lambda e: e.tensor_tensor(out=bq[:, 0, :, :], in0=m1v[:, :, 0, :], in1=m2v[:, :, 1, :], op=ALU.subtract), reads=[rm1, rm2], writes=[rbq])
        S.op("dve", lambda e: e.tensor_tensor(out=bq[:, 1, :, :], in0=m2v[:, :, 0, :], in1=m1v[:, :, 1, :], op=ALU.add), reads=[rm1, rm2], writes=[rbq])
        return bq, rbq

    def phase_hyB(l, seqs):
        with ExitStack() as st:
            S.pst = st
            pgl = TPool(S, "Bgl", [128, 32, 128], BF16, 2); pgo = TPool(S, "Bgo", [128, 32, 2, 128], BF16, 2)
            pA = TPool(S, "BpA", [128, 1024], F32, 1, psum=True); pX = TPool(S, "BpX", [128, 1024], F32, 2, psum=True)
            pm12 = TPool(S, "Bm", [128, 8, 128], F32, 4); pbq = TPool(S, "Bbq", [128, 2, 4, 128], BF16, 2)
            for seq in seqs:
                for o in range(2):
                    for blk in range(HY // 32):
                        gl, rgl = pgl.get(); go, rgo = pgo.get()
                        S.dma("sp", gl[:], gT[seq, o, blk * 32:(blk + 1) * 32, :].rearrange("c (s f) -> s c f", f=128), rgl, writes=[rgl])
                        for g in range(8):
                            psA, rpA = pA.get(); psX, rpX = pX.get()
                            fft_fwd(gl, rgl, 128, g * 4, psA, rpA, psX, rpX, pm12, pbq)
                            S.op("act", act_copy(go[:, g * 4:(g + 1) * 4, 0, :], psX[:, 0:512].rearrange("p (c k) -> p c k", k=128)), reads=[rpX], writes=[rgo])
                            S.op("dve", lambda e: e.tensor_copy(out=go[:, g * 4:(g + 1) * 4, 1, :], in_=psX[:, 512:1024].rearrange("p (c k) -> p c k", k=128)), reads=[rpX], writes=[rgo])
                        S.dma("sp", Gh[seq, o, :, blk * 32:(blk + 1) * 32, :, :], go[:], rgo, reads=[rgo])
        S.end_phase()

    def phase_hyC(l, seqs):
        with ExitStack() as st:
            S.pst = st
            pin = TPool(S, "Cin", [64, 3, 32, 128], BF16, 2); pG = TPool(S, "CG", [128, 2, 32, 2, 128], BF16, 2)
            pout = TPool(S, "Cout", [64, 32, 128], BF16, 2)
            pA = TPool(S, "CpA", [128, 1024], F32, 1, psum=True); pX = TPool(S, "CpX", [128, 1024], F32, 1, psum=True)
            pC = TPool(S, "CpC", [128, 1024], F32, 1, psum=True); pY = TPool(S, "CpY", [128, 512], F32, 2, psum=True)
            pm12 = TPool(S, "Cm", [128, 8, 128], F32, 4); pbq = TPool(S, "Cbq", [128, 2, 4, 128], BF16, 3)
            pp4 = TPool(S, "Cp4", [128, 4, 128], F32, 4); pz = TPool(S, "Cz", [64, 4, 128], BF16, 2)

            def conv(src, rsrc, cl0, G, rG, o, g):
                psA, rpA = pA.get(); psX, rpX = pX.get()
                fft_fwd(src, rsrc, 64, cl0, psA, rpA, psX, rpX, pm12, pbq)
                Xr = psX[:, 0:512].rearrange("p (c k) -> p c k", k=128); Xi = psX[:, 512:1024].rearrange("p (c k) -> p c k", k=128)
                Gr = G[:, o, g * 4:(g + 1) * 4, 0, :]; Gi = G[:, o, g * 4:(g + 1) * 4, 1, :]
                yq, ryq = pbq.get()
                for (a, b, c_, d_, op, ri) in ((Xr, Gr, Xi, Gi, ALU.subtract, 0), (Xr, Gi, Xi, Gr, ALU.add, 1)):
                    p1, r1 = pp4.get(); p2, r2 = pp4.get()
                    S.op("dve", lambda e: e.tensor_tensor(out=p1[:], in0=a, in1=b, op=ALU.mult), reads=[rpX, rG], writes=[r1])
                    S.op("dve", lambda e: e.tensor_tensor(out=p2[:], in0=c_, in1=d_, op=ALU.mult), reads=[rpX, rG], writes=[r2])
                    S.op("dve", lambda e: e.tensor_tensor(out=yq[:, ri, :, :], in0=p1[:], in1=p2[:], op=op), reads=[r1, r2], writes=[ryq])
                psC, rpC = pC.get()

                def i1(pe):
                    for c in range(4):
                        pe.matmul(psC[:, c * 256:(c + 1) * 256], lhsT=yq[:, 0, c, :], rhs=I1A, start=True, stop=False)
                        ins = pe.matmul(psC[:, c * 256:(c + 1) * 256], lhsT=yq[:, 1, c, :], rhs=I1B, start=False, stop=True)
                    return ins
                S.op("pe", i1, reads=[ryq, R_CB], writes=[rpC])
                dq, rdq = twiddle(psC, rpC, TWR, NTWI, pm12, pbq)
                psY, rpY = pY.get()

                def i2(pe):
                    pe.matmul(psY[0:64, :], lhsT=CB[:, I2R, 0:64], rhs=dq[:, 0, :, :].rearrange("p c k -> p (c k)"), start=True, stop=False)
                    return pe.matmul(psY[0:64, :], lhsT=CB[:, I2I, 0:64], rhs=dq[:, 1, :, :].rearrange("p c k -> p (c k)"), start=False, stop=True)
                S.op("pe", i2, reads=[rdq, R_CB], writes=[rpY])
                return psY, rpY

            for seq in seqs:
                for blk in range(HY // 32):
                    xin, rin = pin.get(); G, rG = pG.get(); co, rco = pout.get()
                    for t in range(3):
                        S.dma("sp", xin[:, t, :, :], hyT[seq, t * 512 + blk * 32:t * 512 + (blk + 1) * 32, :].rearrange("c (s f) -> s c f", f=128), rin, writes=[rin])
                    for o in range(2):
                        S.dma("sp", G[:, o, :, :, :], Gh[seq, o, :, blk * 32:(blk + 1) * 32, :, :], rG, writes=[rG])
                    for g in range(8):
                        psY, rpY = conv(xin[:, 0, :, :], rin, g * 4, G, rG, 0, g)
                        z, rz = pz.get()
                        S.op("dve", lambda e: e.tensor_tensor(out=z[:], in0=psY[0:64, :].rearrange("p (c k) -> p c k", k=128), in1=xin[:, 1, g * 4:(g + 1) * 4, :], op=ALU.mult), reads=[rpY, rin], writes=[rz])
                        psY2, rpY2 = conv(z, rz, 0, G, rG, 1, g)
                        S.op("dve", lambda e: e.tensor_tensor(out=co[:, g * 4:(g + 1) * 4, :], in0=psY2[0:64, :].rearrange("p (c k) -> p c k", k=128), in1=xin[:, 2, g * 4:(g + 1) * 4, :], op=ALU.mult), reads=[rpY2, rin], writes=[rco])
                    rows = slice(1024 + blk * 32, 1024 + (blk + 1) * 32)
                    if seq == 0:
                        S.dma("sp", abcT[rows, NC_:U].rearrange("c (s f) -> s c f", f=128), co[:], rco, reads=[rco])
                    else:
                        S.dma("sp", abcT[rows, 0:NC_].rearrange("c (s f) -> s c f", f=128), co[0:2, :, :], rco, reads=[rco])
        S.end_phase()

    return nc, S, locals()


def host_consts():
    C, Sg = rope_tables()
    cb = np.zeros((128, 15, 128), np.float32)
    cb[:, 0, :] = 1.0
    for p in range(128):
        for q in range(128):
            if p // 64 == q // 64:
                cb[p, 1, q] = 1.0 / 64
    cb[:, 2, :] = 1.0 / 128
    for m in range(128):
        k = (m // 64) * 64 + ((m % 64) + 32) % 64
        cb[k, 3, m] = 1.0
    n = np.arange(128)
    ang = 2.0 * np.pi * np.outer(n, n) / 128.0
    Fr, Fi = np.cos(ang), -np.sin(ang)
    cb[:, 4, :] = Fr; cb[:, 5, :] = Fi
    cb[:, 6, :] = Fr; cb[:, 7, :] = Fi; cb[:, 8, :] = -Fi
    M = 16384.0
    cb[:, 9, :] = Fr / M
    cb[:, 10, :] = Fi / M
    cb[:, 11, :] = Fr; cb[:, 12, :] = -Fi
    cb[:, 13, :] = Fi; cb[:, 14, :] = Fr
    cf = np.zeros((128, 6, 128), np.float32)
    cf[:, 0, :] = np.eye(128)
    angt = 2.0 * np.pi * np.outer(n, n) / M
    cf[:, 1, :] = np.cos(angt); cf[:, 2, :] = -np.sin(angt); cf[:, 3, :] = np.sin(angt); cf[:, 4, :] = EPS; cf[:, 5, :] = -math.pi
    zT = np.zeros((2, 2, 33, NL), np.float32); dec = np.zeros((2, 2, HY, NL), np.float32)
    for s, nn in ((0, NL), (1, NC_)):
        z, d = hyena_pos(nn)
        zT[s, 0, :, :nn] = z.T; dec[s, 0, :, :nn] = d.T
        zT[s, 1, :, 0] = z[0]; dec[s, 1, :, 0] = d[0]
        j = np.arange(NL - nn + 1, NL)
        zT[s, 1, :, j] = z[NL - j]; dec[s, 1, :, j] = d[NL - j]
    return dict(ropeC=C, ropeS=Sg, cbf=cb.astype(NPBF), cf32=cf, zT=zT, dec=dec)


_CACHE = {}


def kernel(**inp):
    nl = inp.pop("_nlayers", DEPTH); dbg = inp.pop("_dbg", None); nlw = inp.pop("_nlw", DEPTH); ncores = inp.pop("_ncores", 8)
    key = (nl, dbg, nlw)
    if key not in _CACHE:
        _CACHE[key] = build_all(nl, dbg, nlw)
    nc = _CACHE[key]
    hc = host_consts()
    f = lambda a: np.ascontiguousarray(np.asarray(a, dtype=np.float32))
    shared = {k: f(inp[k][:nlw]) for k in ("w_mod", "b_mod", "w_in", "da_subln", "gq_qn", "gq_kn", "hy_conv_w", "hy_conv_b", "hy_w1", "hy_b1",
                                     "hy_w2", "hy_b2", "hy_w3", "hy_freq", "hy_bias", "w_pa", "w_pb", "w_pc", "w_o", "ln1_g", "ln1_b",
                                     "w_up", "ffn_conv_w", "ffn_conv_b", "w_down", "ln2_g", "ln2_b")}
    shared["lam4"] = np.ascontiguousarray(np.stack([f(inp["da_lq1"]), f(inp["da_lk1"]), f(inp["da_lq2"]), f(inp["da_lk2"])], axis=1)[:nlw])
    shared.update(hc)
    in_maps = []
    for core in range(8):
        b = core // 2
        m = dict(shared)
        m["x"] = f(inp["x"][b]); m["ctx"] = f(inp["ctx"][b])
        m["cvec"] = np.ascontiguousarray(np.stack([f(inp["c"][b]), f(inp["c_ctx"])], axis=0))
        in_maps.append(m)
    res = run_bass_kernel_spmd(nc, in_maps[:ncores], core_ids=list(range(ncores)))
    if dbg:
        return res.results
    return np.stack([np.asarray(res.results[2 * b]["y"], dtype=np.float32) for b in range(4)], axis=0)


FULL_PIPELINE = True


def build_all(nl=DEPTH, dbg=None, nlw=DEPTH):
    nc, S, L = build(nl, dbg, nlw)
    L["phase0"]()
    cur = 0
    if FULL_PIPELINE:
        for l in range(nl):
            last = (l == DEPTH - 1)
            seqs = [0] if last else [0, 1]
            L["phase_w"](l); L["phase_m"](l)
            L["phase1"](l, cur, last)
            L["phase_attn"](l, last)
            L["phase_hyA"](l, seqs); L["phase_hyB"](l, seqs); L["phase_hyC"](l, seqs)
            L["phase3a"](l, cur, last)
            L["phase3b"](l, cur, last)
            cur ^= 1
    L["phase_out"](cur)
    S.reset(final=True)
    return nc
```
